# Optimizing a Trainium2 kernel written in Bass

```python
import jax, jax.numpy as jnp
from jax import lax
import numpy as np

D_MODEL = 1024
BATCH = 8
SEQ = 4096
DEPTH = 2

N_A_LAYERS = DEPTH // 2
N_B_LAYERS = DEPTH - N_A_LAYERS
HEAD_DIM = 64
MIX_WIDTH = D_MODEL
MEM_HEADS = 4
MEM_WIDTH = MEM_HEADS * HEAD_DIM
N_MEM = 256
LRU_WIDTH = MIX_WIDTH - MEM_WIDTH
LRU_BLOCKS = 6
LRU_BLOCK = LRU_WIDTH // LRU_BLOCKS
CONV_WIDTH = 4
LRU_C = 8.0
SWA_Q_HEADS = (MIX_WIDTH - MEM_WIDTH) // HEAD_DIM
SWA_KV_HEADS = 4
SWA_GROUP = SWA_Q_HEADS // SWA_KV_HEADS
KV_WIDTH = SWA_KV_HEADS * HEAD_DIM
WINDOW = 128
BLOCK = 128
ROPE_DIM = HEAD_DIM // 4
ROPE_THETA = 500000.0
D_FF = -(-8 * D_MODEL // (3 * 256)) * 256
A_IN_WIDTH = 2 * LRU_WIDTH + MEM_WIDTH
B_IN_WIDTH = SWA_Q_HEADS * HEAD_DIM + MEM_WIDTH
EPS = 1e-6
NEG_INF = -1e30

kernel_name = "hybrid_rglru_swa_sink_yoco"


def rms_norm(x, g):
    xf = x.astype(jnp.float32)
    y = xf * lax.rsqrt(jnp.mean(xf * xf, axis=-1, keepdims=True) + EPS)
    return (y * g.astype(jnp.float32)).astype(x.dtype)


def rope_tables(seq):
    inv_freq = 1.0 / (ROPE_THETA ** (jnp.arange(0, ROPE_DIM, 2, dtype=jnp.float32) / ROPE_DIM))
    ang = jnp.arange(seq, dtype=jnp.float32)[:, None] * inv_freq[None, :]
    return jnp.cos(ang), jnp.sin(ang)


def apply_partial_rope(x, cos, sin):
    half = ROPE_DIM // 2
    xr = x[..., :ROPE_DIM].astype(jnp.float32)
    x1, x2 = xr[..., :half], xr[..., half:]
    c = cos[None, :, None, :]
    s = sin[None, :, None, :]
    rot = jnp.concatenate([x1 * c - x2 * s, x2 * c + x1 * s], axis=-1).astype(x.dtype)
    return jnp.concatenate([rot, x[..., ROPE_DIM:]], axis=-1)


def causal_depthwise_conv(x, w, b):
    c = x.shape[-1]
    y = lax.conv_general_dilated(
        x, w[:, None, :].astype(x.dtype), window_strides=(1,),
        padding=[(CONV_WIDTH - 1, 0)], dimension_numbers=("NWC", "WIO", "NWC"),
        feature_group_count=c)
    return y + b


def _linear_recurrence_combine(left, right):
    a1, b1 = left
    a2, b2 = right
    return a1 * a2, a2 * b1 + b2


def rg_lru(x, w_r, b_r, w_i, b_i, lam):
    bsz, seq, width = x.shape
    xb = x.reshape(bsz, seq, LRU_BLOCKS, LRU_BLOCK)
    r = jax.nn.sigmoid(jnp.einsum("bshi,hij->bshj", xb, w_r)
                       + b_r.reshape(LRU_BLOCKS, LRU_BLOCK)).reshape(bsz, seq, width)
    i = jax.nn.sigmoid(jnp.einsum("bshi,hij->bshj", xb, w_i)
                       + b_i.reshape(LRU_BLOCKS, LRU_BLOCK)).reshape(bsz, seq, width)
    log_a = LRU_C * r.astype(jnp.float32) * jax.nn.log_sigmoid(lam.astype(jnp.float32))
    a = jnp.exp(log_a)
    mult = jnp.sqrt(jnp.maximum(1.0 - jnp.exp(2.0 * log_a), 0.0))
    first = (jnp.arange(seq) == 0)[None, :, None]
    mult = jnp.where(first, 1.0, mult)
    u = mult * (i * x).astype(jnp.float32)
    _, h = lax.associative_scan(_linear_recurrence_combine, (a, u), axis=1)
    return h.astype(x.dtype)


def sliding_window_attention_with_sinks(q, k, v, sinks):
    bsz, seq = q.shape[0], q.shape[1]
    nb = seq // BLOCK
    qb = q.reshape(bsz, nb, BLOCK, SWA_KV_HEADS, SWA_GROUP, HEAD_DIM)

    def band(t):
        cur = t.reshape(bsz, nb, BLOCK, SWA_KV_HEADS, HEAD_DIM)
        prev = jnp.concatenate([jnp.zeros_like(cur[:, :1]), cur[:, :-1]], axis=1)
        return jnp.concatenate([prev, cur], axis=2)

    kb, vb = band(k), band(v)
    scores = jnp.einsum("bnqkgd,bnckd->bnkgqc", qb, kb).astype(jnp.float32) * (HEAD_DIM ** -0.5)
    qi = jnp.arange(BLOCK)[:, None] + BLOCK
    ci = jnp.arange(2 * BLOCK)[None, :]
    rel = qi - ci
    in_window = (rel >= 0) & (rel < WINDOW)
    has_prev = (jnp.arange(nb) > 0)[:, None, None] | (ci >= BLOCK)[None]
    mask = in_window[None] & has_prev
    scores = jnp.where(mask[None, :, None, None], scores, NEG_INF)
    sink = sinks.astype(jnp.float32).reshape(1, 1, SWA_KV_HEADS, SWA_GROUP, 1, 1)
    m = jnp.maximum(jnp.max(scores, axis=-1, keepdims=True), sink)
    e = jnp.exp(scores - m)
    probs = e / (jnp.sum(e, axis=-1, keepdims=True) + jnp.exp(sink - m))
    out = jnp.einsum("bnkgqc,bnckd->bnqkgd", probs.astype(v.dtype), vb)
    return out.reshape(bsz, seq, SWA_Q_HEADS * HEAD_DIM)


def memory_attention(q, mk, mv):
    bsz, seq = q.shape[0], q.shape[1]
    s = jnp.einsum("bshd,bmhd->bhsm", q, mk).astype(jnp.float32) * (HEAD_DIM ** -0.5)
    p = jax.nn.softmax(s, axis=-1).astype(mv.dtype)
    o = jnp.einsum("bhsm,bmhd->bshd", p, mv)
    return o.reshape(bsz, seq, MEM_WIDTH)


def swiglu_ffn(x, w_in, w_out):
    gu = x @ w_in
    g, u = gu[..., :D_FF], gu[..., D_FF:]
    return (jax.nn.silu(g) * u) @ w_out


def setup_inputs(seed: int = 0) -> dict:
    key = jax.random.key(seed)
    ks = jax.random.split(key, 24)

    def nrm(k, shape, scale):
        return jax.random.normal(k, shape, dtype=jnp.float32) * scale

    def gain(k, shape):
        return 1.0 + 0.05 * jax.random.normal(k, shape, dtype=jnp.float32)

    u = jax.random.uniform(ks[14], (N_A_LAYERS, LRU_WIDTH), dtype=jnp.float32,
                           minval=0.81, maxval=0.998)
    a0 = jnp.sqrt(u)
    lru_lambda = jnp.log(a0) - jnp.log1p(-a0)
    return {
        "x": nrm(ks[0], (BATCH, SEQ, D_MODEL), 1.0),
        "mem": nrm(ks[1], (BATCH, N_MEM, D_MODEL), 1.0),
        "norm_mix_pre": gain(ks[2], (DEPTH, D_MODEL)),
        "norm_mix_post": gain(ks[3], (DEPTH, D_MODEL)),
        "norm_ffn_pre": gain(ks[4], (DEPTH, D_MODEL)),
        "norm_ffn_post": gain(ks[5], (DEPTH, D_MODEL)),
        "mem_norm": gain(ks[6], (D_MODEL,)),
        "w_mem_kv": nrm(ks[7], (DEPTH, D_MODEL, 2 * MEM_WIDTH), D_MODEL ** -0.5),
        "w_in_a": nrm(ks[8], (N_A_LAYERS, D_MODEL, A_IN_WIDTH), D_MODEL ** -0.5),
        "conv_w": nrm(ks[9], (N_A_LAYERS, CONV_WIDTH, LRU_WIDTH), CONV_WIDTH ** -0.5),
        "conv_b": nrm(ks[10], (N_A_LAYERS, LRU_WIDTH), 0.02),
        "w_gate_r": nrm(ks[11], (N_A_LAYERS, LRU_BLOCKS, LRU_BLOCK, LRU_BLOCK), LRU_BLOCK ** -0.5),
        "b_gate_r": nrm(ks[12], (N_A_LAYERS, LRU_WIDTH), 0.02),
        "w_gate_i": nrm(ks[13], (N_A_LAYERS, LRU_BLOCKS, LRU_BLOCK, LRU_BLOCK), LRU_BLOCK ** -0.5),
        "b_gate_i": nrm(ks[15], (N_A_LAYERS, LRU_WIDTH), 0.02),
        "lru_lambda": lru_lambda,
        "norm_kv": gain(ks[16], (D_MODEL,)),
        "w_kv_shared": nrm(ks[17], (D_MODEL, 2 * KV_WIDTH), D_MODEL ** -0.5),
        "w_in_b": nrm(ks[18], (N_B_LAYERS, D_MODEL, B_IN_WIDTH), D_MODEL ** -0.5),
        "sinks": nrm(ks[19], (N_B_LAYERS, SWA_Q_HEADS), 1.0),
        "w_out": nrm(ks[20], (DEPTH, MIX_WIDTH, D_MODEL), MIX_WIDTH ** -0.5),
        "w_ffn_in": nrm(ks[21], (DEPTH, D_MODEL, 2 * D_FF), D_MODEL ** -0.5),
        "w_ffn_out": nrm(ks[22], (DEPTH, D_FF, D_MODEL), D_FF ** -0.5),
    }


def reference(x, mem, norm_mix_pre, norm_mix_post, norm_ffn_pre, norm_ffn_post, mem_norm,
              w_mem_kv, w_in_a, conv_w, conv_b, w_gate_r, b_gate_r, w_gate_i, b_gate_i,
              lru_lambda, norm_kv, w_kv_shared, w_in_b, sinks, w_out, w_ffn_in, w_ffn_out):
    bsz, seq = x.shape[0], x.shape[1]
    n_mem = mem.shape[1]
    cos, sin = rope_tables(seq)
    mem_n = rms_norm(mem, mem_norm)
    h = x
    shared_k = None
    shared_v = None
    for l in range(DEPTH):
        hn = rms_norm(h, norm_mix_pre[l])
        mkv = mem_n @ w_mem_kv[l]
        mk = mkv[..., :MEM_WIDTH].reshape(bsz, n_mem, MEM_HEADS, HEAD_DIM)
        mv = mkv[..., MEM_WIDTH:].reshape(bsz, n_mem, MEM_HEADS, HEAD_DIM)
        if l < N_A_LAYERS:
            a = l
            proj = hn @ w_in_a[a]
            xr = proj[..., :LRU_WIDTH]
            gate = proj[..., LRU_WIDTH:2 * LRU_WIDTH]
            mq = proj[..., 2 * LRU_WIDTH:].reshape(bsz, seq, MEM_HEADS, HEAD_DIM)
            xr = causal_depthwise_conv(xr, conv_w[a], conv_b[a])
            y = rg_lru(xr, w_gate_r[a], b_gate_r[a], w_gate_i[a], b_gate_i[a], lru_lambda[a])
            y = y * jax.nn.gelu(gate)
        else:
            b = l - N_A_LAYERS
            proj = hn @ w_in_b[b]
            q = proj[..., :SWA_Q_HEADS * HEAD_DIM].reshape(bsz, seq, SWA_Q_HEADS, HEAD_DIM)
            mq = proj[..., SWA_Q_HEADS * HEAD_DIM:].reshape(bsz, seq, MEM_HEADS, HEAD_DIM)
            q = apply_partial_rope(q, cos, sin)
            y = sliding_window_attention_with_sinks(q, shared_k, shared_v, sinks[b])
        m_out = memory_attention(mq, mk, mv)
        mixed = jnp.concatenate([y, m_out], axis=-1) @ w_out[l]
        h = h + rms_norm(mixed, norm_mix_post[l])
        f = swiglu_ffn(rms_norm(h, norm_ffn_pre[l]), w_ffn_in[l], w_ffn_out[l])
        h = h + rms_norm(f, norm_ffn_post[l])
        if l == N_A_LAYERS - 1:
            kv = rms_norm(h, norm_kv) @ w_kv_shared
            shared_k = apply_partial_rope(
                kv[..., :KV_WIDTH].reshape(bsz, seq, SWA_KV_HEADS, HEAD_DIM), cos, sin)
            shared_v = kv[..., KV_WIDTH:].reshape(bsz, seq, SWA_KV_HEADS, HEAD_DIM)
    return h
```

```python
from contextlib import ExitStack
import numpy as np
import concourse.bass as bass
import concourse.mybir as mybir
from concourse.bass_utils import run_bass_kernel_spmd

F32 = mybir.dt.float32
BF16 = mybir.dt.bfloat16
AF = mybir.ActivationFunctionType
ALU = mybir.AluOpType

ENGS = ("pe", "act", "dve", "pool", "sp")
D = 1024
SEQ = 4096
NMEM = 256
DFF = 2816
T = 512
EPS = 1e-6
NSLOT = 7
SLOT_ELEMS = 2816
PG = 512
SB_BASE = 16512
SB_END = 229376


class Reg:
    __slots__ = ("writer", "readers", "dma_readers", "excl")

    def __init__(self, excl=False):
        self.excl = excl
        self.writer = None
        self.readers = {}
        self.dma_readers = []


class Op:
    __slots__ = ("eng", "fn", "deps", "sig", "sigval", "dma_sem", "dma_val")


class Prog:
    def __init__(self, nc):
        self.nc = nc
        self.ops = {e: [] for e in ENGS}
        self.dma_counts = {}

    def add(self, eng, fn, reads=(), writes=(), dma_sem=None):
        op = Op()
        op.eng = eng
        op.fn = fn
        op.sig = False
        op.sigval = 0
        op.dma_sem = dma_sem
        op.dma_val = 0
        deps = {}
        writes = list(writes) + [r for r in reads if r.excl]
        reads = [r for r in reads if not r.excl]

        def dep(o, raw):
            if o is None:
                return
            if o.dma_sem is None and o.eng == eng and dma_sem is None and eng == "pe":
                return
            deps[id(o)] = o

        for r in reads:
            dep(r.writer, True)
        for w in writes:
            dep(w.writer, False)
            for o in w.readers.values():
                dep(o, False)
            for o in w.dma_readers:
                dep(o, True)
        op.deps = list(deps.values())
        for r in reads:
            if dma_sem is not None:
                r.dma_readers.append(op)
            else:
                r.readers[eng] = op
        for w in writes:
            w.writer = op
            w.readers = {}
            w.dma_readers = []
        if dma_sem is not None:
            c = self.dma_counts.get(dma_sem, 0) + 16
            self.dma_counts[dma_sem] = c
            op.dma_val = c
        self.ops[eng].append(op)
        return op

    def emit(self, final_waits=()):
        nc = self.nc
        for e in ENGS:
            for op in self.ops[e]:
                for d in op.deps:
                    if d.dma_sem is None:
                        d.sig = True
        for e in ENGS:
            k = 0
            for op in self.ops[e]:
                if op.dma_sem is None and op.sig:
                    k += 1
                    op.sigval = k
        with ExitStack() as es:
            names = ["s_" + e for e in ENGS] + ["d_" + n for n in self.dma_counts]
            tmp = [nc.alloc_semaphore("pre_" + n) for n in names]
            nums = [t.num for t in tmp]
            nc.clear_and_free_semaphores(tmp)
            nc.all_engine_barrier()
            es.enter_context(nc.cleanup_on_exit())
            hs_ = {n: nc.alloc_semaphore(n, num=k) for n, k in zip(names, nums)}
            esem = {e: hs_["s_" + e] for e in ENGS}
            dsem = {n: hs_["d_" + n] for n in self.dma_counts}
            es.callback(nc.all_engine_barrier)
            block = es.enter_context(nc.Block())

            def event(o):
                if o.dma_sem is not None:
                    return dsem[o.dma_sem], o.dma_val, "d_" + o.dma_sem
                return esem[o.eng], o.sigval, "s_" + o.eng

            def run(e, h, extra=()):
                seen = {}
                for op in self.ops[e]:
                    for d in op.deps:
                        sem, val, key = event(d)
                        if seen.get(key, 0) >= val:
                            continue
                        seen[key] = val
                        h.wait_ge(sem, val)
                    ins = op.fn(h)
                    if op.dma_sem is not None:
                        ins.then_inc(dsem[op.dma_sem], 16)
                    elif op.sig:
                        ins.then_inc(esem[e], 1)
                for o in extra:
                    sem, val, key = event(o)
                    if seen.get(key, 0) >= val:
                        continue
                    seen[key] = val
                    h.wait_ge(sem, val)

            @block.tensor
            def _(h):
                run("pe", h)

            @block.scalar
            def _(h):
                run("act", h)

            @block.vector
            def _(h):
                run("dve", h)

            @block.gpsimd
            def _(h):
                run("pool", h)

            @block.sync
            def _(h):
                run("sp", h, final_waits)


class Mem:
    def __init__(self, nc):
        self.nc = nc
        self.top = SB_BASE
        self.pages = {}

    def page(self, k):
        r = self.pages.get(k)
        if r is None:
            r = self.pages[k] = Reg()
        return r

    def alloc(self, name, shape, dtype, at=None):
        esz = 2 if dtype == BF16 else 4
        n = int(np.prod(shape)) * esz
        if at is None:
            at = self.top
            self.top = at + ((n + PG - 1) // PG) * PG
            assert self.top <= SB_END, (name, self.top)
        assert at % 32 == 0 and at + n <= SB_END, (name, at, n)
        return Buf(self, name, list(shape), dtype, at, esz)


class Buf:
    def __init__(self, mem, name, shape, dtype, addr, esz):
        self.mem = mem
        self.t = mem.nc.alloc_sbuf_tensor_at(name, [128] + shape, dtype, offset=addr)
        self.addr = addr
        self.esz = esz
        self.shape = shape
        st = [1] * len(shape)
        for i in range(len(shape) - 2, -1, -1):
            st[i] = st[i + 1] * shape[i + 1]
        self.st = st
        self.nbytes = int(np.prod(shape)) * esz

    def pg(self, *idx):
        lo = 0
        hi = 0
        for i, s in enumerate(self.shape):
            ix = idx[i] if i < len(idx) else None
            if ix is None:
                a, b = 0, s
            elif isinstance(ix, tuple):
                a, b = ix
            else:
                a, b = ix, ix + 1
            lo += a * self.st[i]
            hi += (b - 1) * self.st[i]
        lo_b = self.addr + lo * self.esz
        hi_b = self.addr + (hi + 1) * self.esz
        return [self.mem.page(k) for k in range(lo_b // PG, (hi_b - 1) // PG + 1)]


def _fm(v):
    v = np.asarray(v, np.float32)
    return np.ascontiguousarray(v.reshape(-1, 128).T)


def _kblk(w):
    K, n = w.shape
    return np.ascontiguousarray(w.reshape(K // 128, 128, n).transpose(1, 0, 2)).reshape(128, -1)


def _swap_cols(w):
    n = w.shape[1]
    idx = np.arange(n)
    d = idx % 64
    src = np.where(d < 8, idx + 8, np.where(d < 16, idx - 8, idx))
    return w[:, src]


def _tile_blocks(inp):
    blocks = []
    wa = inp["w_in_a"][0]
    for b in range(7):
        blocks.append(("ina%d" % b, _kblk(wa[:, 256 * b:256 * (b + 1)])))
    g = np.zeros((128, 12, 128), np.float32)
    for j in range(6):
        g[:, j, :] = inp["w_gate_r"][0, j]
        g[:, 6 + j, :] = inp["w_gate_i"][0, j]
    blocks.append(("gates", g.reshape(128, -1)))

    def tail(l):
        wo = inp["w_out"][l]
        for b in range(4):
            blocks.append(("wout%d_%d" % (l, b), _kblk(wo[:, 256 * b:256 * (b + 1)])))
        wi = inp["w_ffn_in"][l]
        for j in range(22):
            blk = np.concatenate([wi[:, 128 * j:128 * (j + 1)], wi[:, DFF + 128 * j:DFF + 128 * (j + 1)]], axis=1)
            blocks.append(("ffin%d_%d" % (l, j), _kblk(blk)))
        wf = inp["w_ffn_out"][l]
        for m in range(8):
            blocks.append(("ffout%d_%d" % (l, m), _kblk(wf[:, 128 * m:128 * (m + 1)])))

    tail(0)
    wkv = inp["w_kv_shared"]
    wk = wkv[:, :256]
    wks = _swap_cols(wk)
    for b in range(2):
        for src, nm in ((wk, "kvk"), (wks, "kvks")):
            cols = []
            for gg in (2 * b, 2 * b + 1):
                cols += [src[:, 64 * gg:64 * (gg + 1)]] * 2
            blocks.append(("%s%d" % (nm, b), _kblk(np.concatenate(cols, axis=1))))
    blocks.append(("kvv", _kblk(wkv[:, 256:512])))
    wb = inp["w_in_b"][0]
    wq = wb[:, :768]
    wqs = _swap_cols(wq)
    for b in range(3):
        blocks.append(("inb%d" % b, _kblk(wq[:, 256 * b:256 * (b + 1)])))
        blocks.append(("inbs%d" % b, _kblk(wqs[:, 256 * b:256 * (b + 1)])))
    blocks.append(("inbm", _kblk(wb[:, 768:1024])))
    tail(1)
    return blocks


def _pro_blocks(inp):
    blocks = []
    for l in range(2):
        blocks.append(("memk%d" % l, _kblk(inp["w_mem_kv"][l][:, :256])))
        blocks.append(("memv%d" % l, _kblk(inp["w_mem_kv"][l][:, 256:512])))
    return blocks


def _block_sizes():
    pro = [("memk0", 2048), ("memv0", 2048), ("memk1", 2048), ("memv1", 2048)]
    tl = [("ina%d" % b, 2048) for b in range(7)] + [("gates", 1536)]

    def tail(l):
        r = [("wout%d_%d" % (l, b), 2048) for b in range(4)]
        r += [("ffin%d_%d" % (l, j), 2048) for j in range(22)]
        r += [("ffout%d_%d" % (l, m), 2816) for m in range(8)]
        return r

    tl += tail(0)
    tl += [("kvk0", 2048), ("kvks0", 2048), ("kvk1", 2048), ("kvks1", 2048), ("kvv", 2048)]
    tl += [x for b in range(3) for x in (("inb%d" % b, 2048), ("inbs%d" % b, 2048))] + [("inbm", 2048)]
    tl += tail(1)
    return pro, tl


CV_MIXPRE = (0, 32)
CV_MIXPOST = (8, 40)
CV_FFNPRE = (16, 48)
CV_FFNPOST = (24, 56)
CV_MEMN = 64
CV_KVN = 72
CV_CONVW = 80
CV_CONVB = 104
CV_BR = 110
CV_BI = 116
CV_LAM = 122
CV_SK = 128
NCV = 134


def _host_consts(inp):
    cv = np.zeros((128, NCV), np.float32)
    for l in range(2):
        cv[:, CV_MIXPRE[l]:CV_MIXPRE[l] + 8] = _fm(inp["norm_mix_pre"][l])
        cv[:, CV_MIXPOST[l]:CV_MIXPOST[l] + 8] = _fm(inp["norm_mix_post"][l])
        cv[:, CV_FFNPRE[l]:CV_FFNPRE[l] + 8] = _fm(inp["norm_ffn_pre"][l])
        cv[:, CV_FFNPOST[l]:CV_FFNPOST[l] + 8] = _fm(inp["norm_ffn_post"][l])
    cv[:, CV_MEMN:CV_MEMN + 8] = _fm(inp["mem_norm"])
    cv[:, CV_KVN:CV_KVN + 8] = _fm(inp["norm_kv"])
    for j in range(4):
        cv[:, CV_CONVW + 6 * j:CV_CONVW + 6 * j + 6] = _fm(inp["conv_w"][0, j])
    cv[:, CV_CONVB:CV_CONVB + 6] = _fm(inp["conv_b"][0])
    cv[:, CV_BR:CV_BR + 6] = _fm(inp["b_gate_r"][0])
    cv[:, CV_BI:CV_BI + 6] = _fm(inp["b_gate_i"][0])
    cv[:, CV_LAM:CV_LAM + 6] = _fm(inp["lru_lambda"][0])
    sk = np.asarray(inp["sinks"][0], np.float32)
    p = np.arange(128)
    for j in range(6):
        cv[:, CV_SK + j] = sk[2 * j + p // 64]
    return cv


def _static_consts(seq):
    ident = np.eye(128, dtype=np.float32)
    c = np.arange(128)[:, None]
    i = np.arange(128)[None, :]
    mprev = (c > i).astype(np.float32)
    mcur = (c <= i).astype(np.float32)
    mask4 = np.concatenate([mprev, mcur, mprev, mcur], axis=1)
    inv_freq = (1.0 / (np.float32(500000.0) ** (np.arange(0, 16, 2, dtype=np.float32) / np.float32(16)))).astype(np.float32)
    ang = (np.arange(seq, dtype=np.float32)[:, None] * inv_freq[None, :]).astype(np.float32)
    cos = np.cos(ang).astype(np.float32).T
    sin = np.sin(ang).astype(np.float32).T
    rc = np.ones((128, seq), np.float32)
    rs = np.zeros((128, seq), np.float32)
    for hh in range(2):
        rc[hh * 64:hh * 64 + 8] = cos
        rc[hh * 64 + 8:hh * 64 + 16] = cos
        rs[hh * 64:hh * 64 + 8] = -sin
        rs[hh * 64 + 8:hh * 64 + 16] = sin
    return ident, mask4, rc, rs


def build(seq=SEQ, stages=("A", "KV", "B")):
    NT = seq // T
    pro_sz, tile_sz = _block_sizes()
    pro_tot = sum(n for _, n in pro_sz) * 128
    tile_tot = sum(n for _, n in tile_sz) * 128

    nc = bass.Bass("TRN2", target_bir_lowering=False)
    x_d = nc.dram_tensor("x", [seq, D], F32, kind="ExternalInput").ap()
    mem_d = nc.dram_tensor("mem", [NMEM, D], F32, kind="ExternalInput").ap()
    cv_d = nc.dram_tensor("cv", [128, NCV], F32, kind="ExternalInput").ap()
    id_d = nc.dram_tensor("ident", [128, 128], F32, kind="ExternalInput").ap()
    mk_d = nc.dram_tensor("mask4", [128, 512], F32, kind="ExternalInput").ap()
    rc_d = nc.dram_tensor("ropec", [128, seq], F32, kind="ExternalInput").ap()
    rs_d = nc.dram_tensor("ropes", [128, seq], F32, kind="ExternalInput").ap()
    wp_d = nc.dram_tensor("wpro", [pro_tot], F32, kind="ExternalInput").ap()
    wt_d = nc.dram_tensor("wtile", [tile_tot], F32, kind="ExternalInput").ap()
    out_d = nc.dram_tensor("out", [seq, D], F32, kind="ExternalOutput").ap()

    P = Prog(nc)
    M = Mem(nc)

    cv = M.alloc("cv", [NCV], F32)
    ident = M.alloc("identb", [128], F32)
    mask4 = M.alloc("mask4b", [512], BF16)
    onesm = M.alloc("onesm", [128], BF16)
    oneslh = M.alloc("oneslh", [2, 128], BF16)
    esk = M.alloc("esk", [6], F32)
    sc8 = M.alloc("sc8", [12], F32)
    tmpc = M.alloc("tmpc", [12], F32)
    state = M.alloc("state", [6], F32)
    mkT = M.alloc("mkT", [2, 2, 256], BF16)
    mvpad = M.alloc("mvpad", [2, 2, 2, 2, 128], BF16)
    h = M.alloc("h", [8, T], F32)
    hn = M.alloc("hn", [8, T], BF16)
    rstd = M.alloc("rstd", [T], F32)
    sqb = M.alloc("sqb", [2, 8, 512], BF16)
    ring = M.alloc("ring", [NSLOT, SLOT_ELEMS], BF16)
    y = M.alloc("y", [8, T], BF16)
    mqT = M.alloc("mqT", [2, T], BF16)
    xin = M.alloc("xin", [2, D], F32)
    xout = M.alloc("xout", [2, D], F32)
    xr = M.alloc("xr", [6, T + 4], F32)
    kT = M.alloc("kT", [4, 128 + T], BF16)
    vpad = M.alloc("vpad", [T // 128 + 1, 4, 2, 128], BF16)
    ropc = M.alloc("ropc", [T], F32)
    rops = M.alloc("rops", [T], F32)
    pm = M.alloc("pm", [2, 4, 512], BF16)
    rec = M.alloc("rec", [2, 512], F32)
    arena = M.top
    act = M.alloc("act", [22, T], BF16)
    mx = M.alloc("mx", [8, T], F32)
    arena_end = M.top
    a0 = arena
    gl = M.alloc("gl", [6, T], F32, at=a0); a0 += gl.nbytes
    xc = M.alloc("xc", [2, T], F32, at=a0); a0 += xc.nbytes
    xcb = M.alloc("xcb", [2, T], BF16, at=a0); a0 += xcb.nbytes
    rr = M.alloc("rr", [2, T], F32, at=a0); a0 += rr.nbytes
    ii = M.alloc("ii", [2, T], F32, at=a0); a0 += ii.nbytes
    aa = M.alloc("aa", [2, T], F32, at=a0); a0 += aa.nbytes
    mm_ = M.alloc("mm", [2, T], F32, at=a0); a0 += mm_.nbytes
    uu = M.alloc("uu", [2, T], F32, at=a0); a0 += uu.nbytes
    hs = M.alloc("hs", [2, T], F32, at=a0); a0 += hs.nbytes
    sg = M.alloc("sg", [2, 512], F32)
    b0 = arena
    qT = M.alloc("qT", [6, T], BF16, at=b0); b0 += qT.nbytes
    rt = M.alloc("rt", [2, 2, 512], F32, at=b0); b0 += rt.nbytes
    assert a0 <= SB_END and b0 <= arena_end + 65536
    M.top = max(M.top, a0, b0)
    assert M.top <= SB_END, M.top

    psb = [nc.alloc_psum_tensor("ps%d" % i, [128, 512], F32) for i in range(8)]
    psr = [Reg(excl=True) for _ in range(8)]
    pctr = [0]

    def bank():
        i = pctr[0] % 8
        pctr[0] += 1
        return psb[i], [psr[i]]

    class Stream:
        def __init__(self, dram, sizes, reps, semprefix):
            self.blocks = []
            for rep in range(reps):
                off = 0
                for nm, n in sizes:
                    self.blocks.append((nm, off, n))
                    off += 128 * n
            self.dram = dram
            self.issued = 0
            self.cur = 0

    streams = [("p", wp_d, pro_sz, 1), ("t", wt_d, tile_sz, NT)]
    allblocks = []
    for key, dram, sizes, reps in streams:
        for rep in range(reps):
            off = 0
            for nm, n in sizes:
                allblocks.append((nm, dram, off, n))
                off += 128 * n
    wstate = {"issued": 0, "cur": 0}

    def wget(name, view):
        i = wstate["cur"]
        nm, dram, off, n = allblocks[i]
        assert nm == name, (nm, name)
        while wstate["issued"] < min(len(allblocks), i + NSLOT - 1):
            j = wstate["issued"]
            nmj, dj, offj, nj = allblocks[j]
            s = j % NSLOT
            P.add("pool", lambda hh, s=s, dj=dj, offj=offj, nj=nj: hh.dma_start(
                out=ring.t[:, s, 0:nj], in_=dj[offj:offj + 128 * nj].rearrange("(p n) -> p n", p=128)),
                writes=ring.pg(s, (0, nj)), dma_sem="w%d" % s)
            wstate["issued"] += 1
        wstate["cur"] += 1
        s = i % NSLOT
        ap = ring.t[:, s, 0:n]
        if view is not None:
            ap = ap.rearrange("p (k c) -> p k c", c=view)
        return ap, ring.pg(s, (0, n))

    def cvc(col):
        return cv.t[:, col:col + 1]

    cvr = cv.pg()

    def A(eng, fn, reads, writes):
        return P.add(eng, fn, reads=reads, writes=writes)

    sq_par = [0]

    def stats_from_sq(par):
        ps, pr = bank()
        for c in range(8):
            A("pe", lambda e, c=c, ps=ps: e.matmul(ps[:, :], onesm.t[:, :], sqb.t[:, par, c, :], start=(c == 0), stop=(c == 7)),
              sqb.pg(par, c) + onesm.pg(), pr)
        A("act", lambda e, ps=ps: e.activation(rstd.t[:, :], ps[:, :], AF.Sqrt, bias=EPS), pr, rstd.pg())
        A("dve", lambda e: e.reciprocal(rstd.t[:, :], rstd.t[:, :]), rstd.pg(), rstd.pg())

    def rmsnorm_to_hn(gcol, n=T, src=None, dst=None):
        src = src or h
        dst = dst or hn
        par = sq_par[0] % 2
        sq_par[0] += 1
        for c in range(8):
            A("act", lambda e, c=c: e.activation(sqb.t[:, par, c, 0:n], src.t[:, c, 0:n], AF.Square),
              src.pg(c, (0, n)), sqb.pg(par, c, (0, n)))
        ps, pr = bank()
        for c in range(8):
            A("pe", lambda e, c=c, ps=ps: e.matmul(ps[:, 0:n], onesm.t[:, :], sqb.t[:, par, c, 0:n], start=(c == 0), stop=(c == 7)),
              sqb.pg(par, c, (0, n)) + onesm.pg(), pr)
        A("act", lambda e, ps=ps: e.activation(rstd.t[:, 0:n], ps[:, 0:n], AF.Sqrt, bias=EPS), pr, rstd.pg((0, n)))
        A("dve", lambda e: e.reciprocal(rstd.t[:, 0:n], rstd.t[:, 0:n]), rstd.pg((0, n)), rstd.pg((0, n)))
        for c in range(8):
            A("dve", lambda e, c=c: e.scalar_tensor_tensor(out=dst.t[:, c, 0:n], in0=src.t[:, c, 0:n], scalar=cvc(gcol + c),
                                                           in1=rstd.t[:, 0:n], op0=ALU.mult, op1=ALU.mult),
              src.pg(c, (0, n)) + rstd.pg((0, n)) + cvr, dst.pg(c, (0, n)))

    class PostNorm:
        def __init__(self, gcol):
            self.gcol = gcol
            self.par = sq_par[0] % 2
            sq_par[0] += 1

        def chunk(self, m, ps, pr):
            par = self.par
            A("act", lambda e: e.activation(sqb.t[:, par, m, :], ps[:, :], AF.Square), pr, sqb.pg(par, m))
            A("dve", lambda e: e.tensor_copy(mx.t[:, m, :], ps[:, :]), pr, mx.pg(m))

        def finish(self):
            stats_from_sq(self.par)
            for c in range(8):
                A("dve", lambda e, c=c: e.scalar_tensor_tensor(out=mx.t[:, c, :], in0=mx.t[:, c, :], scalar=cvc(self.gcol + c),
                                                               in1=rstd.t[:, :], op0=ALU.mult, op1=ALU.mult),
                  mx.pg(c) + rstd.pg() + cvr, mx.pg(c))
                A("pool", lambda e, c=c: e.tensor_tensor(out=h.t[:, c, :], in0=h.t[:, c, :], in1=mx.t[:, c, :], op=ALU.add),
                  h.pg(c) + mx.pg(c), h.pg(c))

    A_dma = lambda eng, fn, reads, writes, sem: P.add(eng, fn, reads=reads, writes=writes, dma_sem=sem)
    A_dma("sp", lambda e: e.dma_start(out=cv.t[:, :], in_=cv_d), [], cv.pg(), "c0")
    A_dma("sp", lambda e: e.dma_start(out=ident.t[:, :], in_=id_d), [], ident.pg(), "c1")
    A_dma("pool", lambda e: e.dma_start(out=mask4.t[:, :], in_=mk_d), [], mask4.pg(), "c2")
    A("dve", lambda e: e.memset(onesm.t[:, :], 1.0 / D), [], onesm.pg())
    A("dve", lambda e: e.memset(oneslh.t[:, :, :], 0.0), [], oneslh.pg())
    A("dve", lambda e: e.memset(oneslh.t[:, 0, 0:64], 1.0), [], oneslh.pg())
    A("dve", lambda e: e.memset(oneslh.t[:, 1, 64:128], 1.0), [], oneslh.pg())
    A("dve", lambda e: e.memset(state.t[:, :], 0.0), [], state.pg())
    A("dve", lambda e: e.memset(xr.t[:, :, 0:4], 0.0), [], xr.pg())
    A("dve", lambda e: e.memset(vpad.t[:, :, :, :, :], 0.0), [], vpad.pg())
    A("dve", lambda e: e.memset(kT.t[:, :, :], 0.0), [], kT.pg())
    A("dve", lambda e: e.memset(mvpad.t[:, :, :, :, :, :], 0.0), [], mvpad.pg())
    A("act", lambda e: e.activation(esk.t[:, :], cv.t[:, CV_SK:CV_SK + 6], AF.Exp), cvr, esk.pg())
    A("act", lambda e: e.activation(tmpc.t[:, 0:6], cv.t[:, CV_LAM:CV_LAM + 6], AF.Exp, scale=-1.0), cvr, tmpc.pg())
    A("act", lambda e: e.activation(tmpc.t[:, 6:12], tmpc.t[:, 0:6], AF.Ln, bias=1.0), tmpc.pg(), tmpc.pg())
    A("dve", lambda e: e.tensor_scalar(sc8.t[:, 0:6], tmpc.t[:, 6:12], -8.0, None, op0=ALU.mult), tmpc.pg(), sc8.pg())
    A("dve", lambda e: e.tensor_scalar(sc8.t[:, 6:12], tmpc.t[:, 6:12], -16.0, None, op0=ALU.mult), tmpc.pg(), sc8.pg())

    for mb in range(2):
        A_dma("sp", lambda e, mb=mb: e.dma_start(out=xin.t[:, mb, :], in_=mem_d[mb * 128:(mb + 1) * 128, :]), [], xin.pg(mb), "xi%d" % mb)
        for cg in range(2):
            ps, pr = bank()
            for q in range(4):
                c = cg * 4 + q
                A("pe", lambda e, ps=ps, q=q, c=c, mb=mb: e.transpose(ps[:, q * 128:(q + 1) * 128], xin.t[:, mb, c * 128:(c + 1) * 128], ident.t[:, :]),
                  xin.pg(mb, (c * 128, (c + 1) * 128)) + ident.pg(), pr)
            A("dve", lambda e, ps=ps, cg=cg, mb=mb: e.tensor_copy(h.t[:, cg * 4:(cg + 1) * 4, mb * 128:(mb + 1) * 128],
                                                                 ps[:, :].rearrange("p (c t) -> p c t", c=4)),
              pr, h.pg((cg * 4, cg * 4 + 4), (mb * 128, (mb + 1) * 128)))
    rmsnorm_to_hn(CV_MEMN, n=NMEM)
    for l in range(2):
        wk, wkr = wget("memk%d" % l, 256)
        for pair in range(2):
            ps, pr = bank()
            for k in range(8):
                A("pe", lambda e, ps=ps, k=k, wk=wk, pair=pair: e.matmul(ps[:, 0:NMEM], wk[:, k, pair * 128:(pair + 1) * 128], hn.t[:, k, 0:NMEM],
                                                                       start=(k == 0), stop=(k == 7)),
                  wkr + hn.pg(k, (0, NMEM)), pr)
            A("act", lambda e, ps=ps, l=l, pair=pair: e.activation(mkT.t[:, l, pair, :], ps[:, 0:NMEM], AF.Copy), pr, mkT.pg(l, pair))
        wv, wvr = wget("memv%d" % l, 256)
        for mc in range(2):
            ps, pr = bank()
            for k in range(8):
                A("pe", lambda e, ps=ps, k=k, wv=wv, mc=mc: e.matmul(ps[:, 0:256], hn.t[:, k, mc * 128:(mc + 1) * 128], wv[:, k, :],
                                                                   start=(k == 0), stop=(k == 7)),
                  wvr + hn.pg(k, (mc * 128, (mc + 1) * 128)), pr)
            for e2 in range(2):
                A("dve" if e2 == 0 else "act",
                  (lambda e, ps=ps, l=l, mc=mc: e.tensor_copy(
                      mvpad.t[:, l, mc, :, 0, 0:64], ps[:, 0:256].rearrange("p (j e d) -> p j e d", j=2, e=2)[:, :, 0, :])) if e2 == 0 else
                  (lambda e, ps=ps, l=l, mc=mc: e.activation(
                      mvpad.t[:, l, mc, :, 1, 64:128], ps[:, 0:256].rearrange("p (j e d) -> p j e d", j=2, e=2)[:, :, 1, :], AF.Copy)),
                  pr, mvpad.pg(l, mc))

    def mem_attention(l):
        def one_pair(j):
            par = j % 2
            for e2 in range(2):
                base = e2 * 64
                for mc in range(2):
                    ps, pr = bank()
                    A("pe", lambda e, ps=ps, base=base, mc=mc, j=j: e.matmul(
                        ps[:, :], mkT.t[base:base + 64, l, j, mc * 128:(mc + 1) * 128], mqT.t[base:base + 64, j, :], start=True, stop=True),
                      mkT.pg(l, j) + mqT.pg(j), pr)
                    A("act", lambda e, ps=ps, e2=e2, mc=mc: e.activation(pm.t[:, par, e2 * 2 + mc, :], ps[:, :], AF.Exp, scale=0.125),
                      pr, pm.pg(par, e2 * 2 + mc))
            pso, pro_ = bank()
            psd, prd = bank()
            for idx in range(4):
                e2, mc = idx // 2, idx % 2
                A("pe", lambda e, pso=pso, e2=e2, mc=mc, idx=idx, j=j: e.matmul(
                    pso[:, :], mvpad.t[:, l, mc, j, e2, :], pm.t[:, par, e2 * 2 + mc, :], start=(idx == 0), stop=(idx == 3)),
                  mvpad.pg(l, mc) + pm.pg(par, e2 * 2 + mc), pro_)
            for idx in range(4):
                e2, mc = idx // 2, idx % 2
                A("pe", lambda e, psd=psd, e2=e2, mc=mc, idx=idx: e.matmul(
                    psd[:, :], oneslh.t[:, e2, :], pm.t[:, par, e2 * 2 + mc, :], start=(idx == 0), stop=(idx == 3)),
                  oneslh.pg() + pm.pg(par, e2 * 2 + mc), prd)
            A("dve", lambda e, psd=psd: e.reciprocal(rec.t[:, par, :], psd[:, :]), prd, rec.pg(par))
            A("dve", lambda e, pso=pso, j=j: e.tensor_tensor(out=y.t[:, 6 + j, :], in0=pso[:, :], in1=rec.t[:, par, :], op=ALU.mult),
              pro_ + rec.pg(par), y.pg(6 + j))

        for j in range(2):
            one_pair(j)

    def out_proj_and_ffn(l):
        pn = PostNorm(CV_MIXPOST[l])
        for b in range(4):
            wb, wr = wget("wout%d_%d" % (l, b), 256)
            for mi in range(2):
                m = 2 * b + mi
                ps, pr = bank()
                for k in range(8):
                    A("pe", lambda e, ps=ps, k=k, wb=wb, mi=mi: e.matmul(ps[:, :], wb[:, k, mi * 128:(mi + 1) * 128], y.t[:, k, :],
                                                                       start=(k == 0), stop=(k == 7)),
                      wr + y.pg(k), pr)
                pn.chunk(m, ps, pr)
        pn.finish()
        rmsnorm_to_hn(CV_FFNPRE[l])
        for j in range(22):
            wb, wr = wget("ffin%d_%d" % (l, j), 256)
            psg, prg = bank()
            psu, pru = bank()
            for k in range(8):
                A("pe", lambda e, psg=psg, k=k, wb=wb: e.matmul(psg[:, :], wb[:, k, 0:128], hn.t[:, k, :], start=(k == 0), stop=(k == 7)),
                  wr + hn.pg(k), prg)
            for k in range(8):
                A("pe", lambda e, psu=psu, k=k, wb=wb: e.matmul(psu[:, :], wb[:, k, 128:256], hn.t[:, k, :], start=(k == 0), stop=(k == 7)),
                  wr + hn.pg(k), pru)
            par = j % 2
            A("act", lambda e, psg=psg, par=par: e.activation(sg.t[:, par, :], psg[:, :], AF.Silu), prg, sg.pg(par))
            A("dve", lambda e, psu=psu, par=par, j=j: e.tensor_tensor(out=act.t[:, j, :], in0=psu[:, :], in1=sg.t[:, par, :], op=ALU.mult),
              pru + sg.pg(par), act.pg(j))
        pn = PostNorm(CV_FFNPOST[l])
        for m in range(8):
            wb, wr = wget("ffout%d_%d" % (l, m), 128)
            ps, pr = bank()
            for k in range(22):
                A("pe", lambda e, ps=ps, k=k, wb=wb: e.matmul(ps[:, :], wb[:, k, :], act.t[:, k, :], start=(k == 0), stop=(k == 21)),
                  wr + act.pg(k), pr)
            pn.chunk(m, ps, pr)
        pn.finish()

    def rope_evac(ps, pr, pss, prs, dst_ap, dst_regs, slot):
        A("dve", lambda e: e.tensor_tensor(out=rt.t[:, slot, 0, :], in0=ps[:, :], in1=ropc.t[:, :], op=ALU.mult),
          pr + ropc.pg(), rt.pg(slot, 0))
        A("dve", lambda e: e.tensor_tensor(out=rt.t[:, slot, 1, :], in0=pss[:, :], in1=rops.t[:, :], op=ALU.mult),
          prs + rops.pg(), rt.pg(slot, 1))
        A("pool", lambda e: e.tensor_tensor(out=dst_ap, in0=rt.t[:, slot, 0, :], in1=rt.t[:, slot, 1, :], op=ALU.add),
          rt.pg(slot, 0) + rt.pg(slot, 1), dst_regs)

    out_ops = []

    for ti in range(NT):
        t0 = ti * T
        for blk in range(4):
            sl = blk % 2
            A_dma("sp", lambda e, sl=sl, blk=blk, t0=t0: e.dma_start(out=xin.t[:, sl, :], in_=x_d[t0 + blk * 128:t0 + (blk + 1) * 128, :]),
                  [], xin.pg(sl), "xi%d" % sl)
            for cg in range(2):
                ps, pr = bank()
                for q in range(4):
                    c = cg * 4 + q
                    A("pe", lambda e, ps=ps, q=q, c=c, sl=sl: e.transpose(ps[:, q * 128:(q + 1) * 128], xin.t[:, sl, c * 128:(c + 1) * 128], ident.t[:, :]),
                      xin.pg(sl, (c * 128, (c + 1) * 128)) + ident.pg(), pr)
                A("dve" if cg == 0 else "act",
                  (lambda e, ps=ps, cg=cg, blk=blk: e.tensor_copy(h.t[:, cg * 4:(cg + 1) * 4, blk * 128:(blk + 1) * 128],
                                                                  ps[:, :].rearrange("p (c t) -> p c t", c=4))) if cg == 0 else
                  (lambda e, ps=ps, cg=cg, blk=blk: e.activation(h.t[:, cg * 4:(cg + 1) * 4, blk * 128:(blk + 1) * 128],
                                                                 ps[:, :].rearrange("p (c t) -> p c t", c=4), AF.Copy)),
                  pr, h.pg((cg * 4, cg * 4 + 4), (blk * 128, (blk + 1) * 128)))
        A_dma("sp", lambda e, t0=t0: e.dma_start(out=ropc.t[:, :], in_=rc_d[:, t0:t0 + T]), [], ropc.pg(), "rc")
        A_dma("sp", lambda e, t0=t0: e.dma_start(out=rops.t[:, :], in_=rs_d[:, t0:t0 + T]), [], rops.pg(), "rs")

        if any(st.startswith("A") for st in stages):
            fullA = "A" in stages
            rmsnorm_to_hn(CV_MIXPRE[0])
            for b in range(7):
                wb, wr = wget("ina%d" % b, 256)
                for mi in range(2):
                    cc = 2 * b + mi
                    ps, pr = bank()
                    for k in range(8):
                        A("pe", lambda e, ps=ps, k=k, wb=wb, mi=mi: e.matmul(ps[:, :], wb[:, k, mi * 128:(mi + 1) * 128], hn.t[:, k, :],
                                                                           start=(k == 0), stop=(k == 7)),
                          wr + hn.pg(k), pr)
                    if cc < 6:
                        A("act", lambda e, ps=ps, cc=cc: e.activation(xr.t[:, cc, 4:4 + T], ps[:, :], AF.Copy), pr, xr.pg(cc, (4, 4 + T)))
                    elif cc < 12:
                        A("act", lambda e, ps=ps, cc=cc: e.activation(gl.t[:, cc - 6, :], ps[:, :], AF.Gelu_apprx_tanh), pr, gl.pg(cc - 6))
                    else:
                        A("act", lambda e, ps=ps, cc=cc: e.activation(mqT.t[:, cc - 12, :], ps[:, :], AF.Copy), pr, mqT.pg(cc - 12))
            wg, wgr = wget("gates", 128)
            for c in range(6 if (fullA or "A2" in stages) else 0):
                par = c % 2
                A("dve", lambda e, c=c, par=par: e.tensor_scalar(xc.t[:, par, :], xr.t[:, c, 4:4 + T], cvc(CV_CONVW + 18 + c), cvc(CV_CONVB + c),
                                                                op0=ALU.mult, op1=ALU.add),
                  xr.pg(c) + cvr, xc.pg(par))
                for jj in range(3):
                    A("dve", lambda e, c=c, par=par, jj=jj: e.scalar_tensor_tensor(out=xc.t[:, par, :], in0=xr.t[:, c, 1 + jj:1 + jj + T],
                                                                                   scalar=cvc(CV_CONVW + 6 * jj + c), in1=xc.t[:, par, :],
                                                                                   op0=ALU.mult, op1=ALU.add),
                      xr.pg(c) + xc.pg(par) + cvr, xc.pg(par))
                A("dve", lambda e, c=c: e.tensor_copy(xr.t[:, c, 1:4], xr.t[:, c, T + 1:T + 4]), xr.pg(c), xr.pg(c))
                A("act", lambda e, par=par: e.activation(xcb.t[:, par, :], xc.t[:, par, :], AF.Copy), xc.pg(par), xcb.pg(par))
                psr_, prr = bank()
                psi, pri = bank()
                A("pe", lambda e, psr_=psr_, c=c, par=par, wg=wg: e.matmul(psr_[:, :], wg[:, c, :], xcb.t[:, par, :], start=True, stop=True),
                  wgr + xcb.pg(par), prr)
                A("pe", lambda e, psi=psi, c=c, par=par, wg=wg: e.matmul(psi[:, :], wg[:, 6 + c, :], xcb.t[:, par, :], start=True, stop=True),
                  wgr + xcb.pg(par), pri)
                A("act", lambda e, psr_=psr_, c=c, par=par: e.activation(rr.t[:, par, :], psr_[:, :], AF.Sigmoid, bias=cvc(CV_BR + c)),
                  prr + cvr, rr.pg(par))
                A("act", lambda e, psi=psi, c=c, par=par: e.activation(ii.t[:, par, :], psi[:, :], AF.Sigmoid, bias=cvc(CV_BI + c)),
                  pri + cvr, ii.pg(par))
                A("act", lambda e, c=c, par=par: e.activation(aa.t[:, par, :], rr.t[:, par, :], AF.Exp, scale=sc8.t[:, c:c + 1]),
                  rr.pg(par) + sc8.pg(), aa.pg(par))
                A("act", lambda e, c=c, par=par: e.activation(mm_.t[:, par, :], rr.t[:, par, :], AF.Exp, scale=sc8.t[:, 6 + c:7 + c]),
                  rr.pg(par) + sc8.pg(), mm_.pg(par))
                A("dve", lambda e, par=par: e.tensor_scalar(mm_.t[:, par, :], mm_.t[:, par, :], 1.0, -1.0, op0=ALU.min, op1=ALU.mult),
                  mm_.pg(par), mm_.pg(par))
                A("act", lambda e, par=par: e.activation(mm_.t[:, par, :], mm_.t[:, par, :], AF.Sqrt, bias=1.0), mm_.pg(par), mm_.pg(par))
                if ti == 0:
                    A("dve", lambda e, par=par: e.memset(mm_.t[:, par, 0:1], 1.0), [], mm_.pg(par))
                A("dve", lambda e, par=par: e.tensor_tensor(out=uu.t[:, par, :], in0=ii.t[:, par, :], in1=xc.t[:, par, :], op=ALU.mult),
                  ii.pg(par) + xc.pg(par), uu.pg(par))
                A("dve", lambda e, par=par: e.tensor_tensor(out=uu.t[:, par, :], in0=uu.t[:, par, :], in1=mm_.t[:, par, :], op=ALU.mult),
                  uu.pg(par) + mm_.pg(par), uu.pg(par))
                A("dve", lambda e, c=c, par=par: e.tensor_tensor_scan(hs.t[:, par, :], aa.t[:, par, :], uu.t[:, par, :], state.t[:, c:c + 1],
                                                                      op0=ALU.mult, op1=ALU.add),
                  aa.pg(par) + uu.pg(par) + state.pg(), hs.pg(par))
                A("dve", lambda e, c=c, par=par: e.tensor_copy(state.t[:, c:c + 1], hs.t[:, par, T - 1:T]), hs.pg(par), state.pg())
                A("dve", lambda e, c=c, par=par: e.tensor_tensor(out=y.t[:, c, :], in0=hs.t[:, par, :], in1=gl.t[:, c, :], op=ALU.mult),
                  hs.pg(par) + gl.pg(c), y.pg(c))
            if fullA or "A3" in stages:
                mem_attention(0)
            if fullA or "A4" in stages:
                out_proj_and_ffn(0)
            else:
                for nm, n in tile_sz[8:8 + 34]:
                    wget(nm, None)
        else:
            for nm, n in tile_sz[:8 + 34]:
                wget(nm, None)

        if "KV" in stages:
            rmsnorm_to_hn(CV_KVN)
            for b in range(2):
                wk, wkr = wget("kvk%d" % b, 256)
                wks, wksr = wget("kvks%d" % b, 256)
                for gi in range(2):
                    g = 2 * b + gi
                    ps, pr = bank()
                    pss, prs = bank()
                    for k in range(8):
                        A("pe", lambda e, ps=ps, k=k, wk=wk, gi=gi: e.matmul(ps[:, :], wk[:, k, gi * 128:(gi + 1) * 128], hn.t[:, k, :],
                                                                           start=(k == 0), stop=(k == 7)), wkr + hn.pg(k), pr)
                    for k in range(8):
                        A("pe", lambda e, pss=pss, k=k, wks=wks, gi=gi: e.matmul(pss[:, :], wks[:, k, gi * 128:(gi + 1) * 128], hn.t[:, k, :],
                                                                               start=(k == 0), stop=(k == 7)), wksr + hn.pg(k), prs)
                    rope_evac(ps, pr, pss, prs, kT.t[:, g, 128:128 + T], kT.pg(g, (128, 128 + T)), g % 2)
            wv, wvr = wget("kvv", 256)
            for blk in range(4):
                ps, pr = bank()
                for k in range(8):
                    A("pe", lambda e, ps=ps, k=k, blk=blk, wv=wv: e.matmul(ps[:, 0:256], hn.t[:, k, blk * 128:(blk + 1) * 128], wv[:, k, :],
                                                                  start=(k == 0), stop=(k == 7)),
                      wvr + hn.pg(k, (blk * 128, (blk + 1) * 128)), pr)
                A("dve", lambda e, ps=ps, blk=blk: e.tensor_copy(vpad.t[:, 1 + blk, :, 0, 0:64], ps[:, 0:256].rearrange("p (g d) -> p g d", d=64)),
                  pr, vpad.pg(1 + blk))
                A("act", lambda e, ps=ps, blk=blk: e.activation(vpad.t[:, 1 + blk, :, 1, 64:128], ps[:, 0:256].rearrange("p (g d) -> p g d", d=64), AF.Copy),
                  pr, vpad.pg(1 + blk))
        else:
            for nm in ("kvk0", "kvks0", "kvk1", "kvks1", "kvv"):
                wget(nm, None)

        if "B" in stages:
            rmsnorm_to_hn(CV_MIXPRE[1])
            for b in range(3):
                wq, wqr = wget("inb%d" % b, 256)
                wqs, wqsr = wget("inbs%d" % b, 256)
                for mi in range(2):
                    c = 2 * b + mi
                    ps, pr = bank()
                    pss, prs = bank()
                    for k in range(8):
                        A("pe", lambda e, ps=ps, k=k, wq=wq, mi=mi: e.matmul(ps[:, :], wq[:, k, mi * 128:(mi + 1) * 128], hn.t[:, k, :],
                                                                           start=(k == 0), stop=(k == 7)), wqr + hn.pg(k), pr)
                    for k in range(8):
                        A("pe", lambda e, pss=pss, k=k, wqs=wqs, mi=mi: e.matmul(pss[:, :], wqs[:, k, mi * 128:(mi + 1) * 128], hn.t[:, k, :],
                                                                               start=(k == 0), stop=(k == 7)), wqsr + hn.pg(k), prs)
                    rope_evac(ps, pr, pss, prs, qT.t[:, c, :], qT.pg(c), c % 2)
            wm, wmr = wget("inbm", 256)
            for mi in range(2):
                ps, pr = bank()
                for k in range(8):
                    A("pe", lambda e, ps=ps, k=k, mi=mi, wm=wm: e.matmul(ps[:, :], wm[:, k, mi * 128:(mi + 1) * 128], hn.t[:, k, :],
                                                                start=(k == 0), stop=(k == 7)), wmr + hn.pg(k), pr)
                A("act", lambda e, ps=ps, mi=mi: e.activation(mqT.t[:, mi, :], ps[:, :], AF.Copy), pr, mqT.pg(mi))
            it = 0
            for blk in range(4):
                first = (ti == 0 and blk == 0)
                chs = (1,) if first else (0, 1)
                for j in range(6):
                    par = it % 2
                    it += 1
                    for e2 in range(2):
                        ps, pr = bank()
                        hh = 2 * j + e2
                        g = hh // 3
                        base = e2 * 64
                        for ch in chs:
                            kc0 = blk * 128 + ch * 128
                            A("pe", lambda e, ps=ps, base=base, g=g, kc0=kc0, j=j, blk=blk, ch=ch: e.matmul(
                                ps[:, ch * 128:(ch + 1) * 128], kT.t[base:base + 64, g, kc0:kc0 + 128],
                                qT.t[base:base + 64, j, blk * 128:(blk + 1) * 128], start=True, stop=True),
                              kT.pg(g, (kc0, kc0 + 128)) + qT.pg(j, (blk * 128, (blk + 1) * 128)), pr)
                        if first:
                            A("dve", lambda e, par=par, e2=e2: e.memset(pm.t[:, par, 0, e2 * 256:e2 * 256 + 128], 0.0), [], pm.pg(par, 0))
                            A("act", lambda e, ps=ps, par=par, e2=e2: e.activation(pm.t[:, par, 0, e2 * 256 + 128:(e2 + 1) * 256],
                                                                               ps[:, 128:256], AF.Exp, scale=0.125),
                              pr, pm.pg(par, 0))
                        else:
                            A("act", lambda e, ps=ps, par=par, e2=e2: e.activation(pm.t[:, par, 0, e2 * 256:(e2 + 1) * 256], ps[:, 0:256],
                                                                               AF.Exp, scale=0.125), pr, pm.pg(par, 0))
                    A("dve", lambda e, par=par: e.tensor_tensor(out=pm.t[:, par, 1, :], in0=pm.t[:, par, 0, :], in1=mask4.t[:, :], op=ALU.mult),
                      pm.pg(par, 0) + mask4.pg(), pm.pg(par, 1))
                    pso, pro_ = bank()
                    n_acc = 2 * len(chs)
                    idx = 0
                    for e2 in range(2):
                        g = (2 * j + e2) // 3
                        for ch in chs:
                            A("pe", lambda e, pso=pso, e2=e2, ch=ch, g=g, blk=blk, par=par, idx=idx, n_acc=n_acc: e.matmul(
                                pso[:, 0:128], vpad.t[:, blk + ch, g, e2, :], pm.t[:, par, 1, (e2 * 2 + ch) * 128:(e2 * 2 + ch + 1) * 128],
                                start=(idx == 0), stop=(idx == n_acc - 1)),
                              vpad.pg(blk + ch) + pm.pg(par, 1), pro_)
                            idx += 1
                    idx = 0
                    for e2 in range(2):
                        for ch in chs:
                            A("pe", lambda e, pso=pso, e2=e2, ch=ch, par=par, idx=idx, n_acc=n_acc: e.matmul(
                                pso[:, 128:256], oneslh.t[:, e2, :], pm.t[:, par, 1, (e2 * 2 + ch) * 128:(e2 * 2 + ch + 1) * 128],
                                start=(idx == 0), stop=(idx == n_acc - 1)),
                              oneslh.pg() + pm.pg(par, 1), pro_)
                            idx += 1
                    A("dve", lambda e, pso=pso, par=par, j=j: e.tensor_scalar(rec.t[:, par, 0:128], pso[:, 128:256], esk.t[:, j:j + 1], None, op0=ALU.add),
                      pro_ + esk.pg(), rec.pg(par))
                    A("dve", lambda e, par=par: e.reciprocal(rec.t[:, par, 0:128], rec.t[:, par, 0:128]), rec.pg(par), rec.pg(par))
                    A("dve", lambda e, pso=pso, par=par, j=j, blk=blk: e.tensor_tensor(out=y.t[:, j, blk * 128:(blk + 1) * 128], in0=pso[:, 0:128],
                                                                                      in1=rec.t[:, par, 0:128], op=ALU.mult),
                      pro_ + rec.pg(par), y.pg(j, (blk * 128, (blk + 1) * 128)))
            A("dve", lambda e: e.tensor_copy(kT.t[:, :, 0:128], kT.t[:, :, T:T + 128]), kT.pg(), kT.pg())
            A("dve", lambda e: e.tensor_copy(vpad.t[:, 0, :, :, :], vpad.t[:, T // 128, :, :, :]), vpad.pg(T // 128), vpad.pg(0))
            mem_attention(1)
            out_proj_and_ffn(1)
        else:
            for nm, n in tile_sz[8 + 34 + 5:]:
                wget(nm, None)

        for blk in range(4):
            sl = blk % 2
            for cg in range(2):
                ps, pr = bank()
                for q in range(4):
                    c = cg * 4 + q
                    A("pe", lambda e, ps=ps, q=q, c=c, blk=blk: e.transpose(ps[:, q * 128:(q + 1) * 128], h.t[:, c, blk * 128:(blk + 1) * 128], ident.t[:, :]),
                      h.pg(c, (blk * 128, (blk + 1) * 128)) + ident.pg(), pr)
                A("dve" if cg == 0 else "act",
                  (lambda e, ps=ps, sl=sl, cg=cg: e.tensor_copy(xout.t[:, sl, cg * 512:(cg + 1) * 512], ps[:, :])) if cg == 0 else
                  (lambda e, ps=ps, sl=sl, cg=cg: e.activation(xout.t[:, sl, cg * 512:(cg + 1) * 512], ps[:, :], AF.Copy)),
                  pr, xout.pg(sl, (cg * 512, (cg + 1) * 512)))
            out_ops.append(A_dma("sp", lambda e, sl=sl, blk=blk, t0=t0: e.dma_start(out=out_d[t0 + blk * 128:t0 + (blk + 1) * 128, :], in_=xout.t[:, sl, :]),
                                 xout.pg(sl), [], "xo%d" % sl))

    assert wstate["cur"] == len(allblocks), (wstate["cur"], len(allblocks))
    P.emit(final_waits=out_ops[-2:])
    return nc


_CACHE = {}


def prep_inputs(inputs, seq=SEQ):
    inp = {k: np.asarray(v, np.float32) for k, v in inputs.items()}
    cv = _host_consts(inp)
    ident, mask4, rc, rs = _static_consts(seq)
    wpro = np.concatenate([b.reshape(-1) for _, b in _pro_blocks(inp)])
    wtile = np.concatenate([b.reshape(-1) for _, b in _tile_blocks(inp)])
    shared = {"cv": cv, "ident": ident, "mask4": mask4, "ropec": rc, "ropes": rs, "wpro": wpro, "wtile": wtile}
    return inp, shared


def kernel(**inputs):
    inp, shared = prep_inputs(inputs)
    B = inp["x"].shape[0]
    if "nc" not in _CACHE:
        _CACHE["nc"] = build()
    nc = _CACHE["nc"]
    in_maps = []
    for b in range(B):
        m = dict(shared)
        m["x"] = np.ascontiguousarray(inp["x"][b])
        m["mem"] = np.ascontiguousarray(inp["mem"][b])
        in_maps.append(m)
    res = run_bass_kernel_spmd(nc, in_maps, core_ids=list(range(B)))
    return np.stack([np.asarray(r["out"], np.float32) for r in res.results], axis=0)
```
